# Optimizing a Trainium2 kernel written in Bass

```python
import math
import jax, jax.numpy as jnp
from jax import lax
import numpy as np


D_MODEL = 2048
BATCH = 8
SEQ = 2048
DEPTH = 2

MEM_LEN = 256
D_MIX = D_MODEL
M_HEADS = 4
M_DV = D_MIX // 2 // M_HEADS
M_DQK = M_DV // 2
CONV_K = 4
CHUNK = 64
DA_HEADS = 8
DA_DV = D_MIX // 2 // DA_HEADS
DA_D = DA_DV // 2
Q_BLOCK = 128
NUM_BUCKETS = 32
MAX_EXACT = NUM_BUCKETS // 2
MAX_DISTANCE = 128
X_HEADS = 4
X_DH = D_MODEL // X_HEADS
D_FF = ((8 * D_MODEL // 3 + 255) // 256) * 256
EPS = 1e-6

M_QK = 2 * M_HEADS * M_DQK
M_V = M_HEADS * M_DV
M_O = M_HEADS * M_DV
M_G = 2 * M_HEADS
DA_QK = 2 * DA_HEADS * 2 * DA_D
DA_V = DA_HEADS * DA_DV
N_IN = M_QK + M_V + M_O + M_G + DA_QK + DA_V
SPLITS = [int(s) for s in np.cumsum([M_QK, M_V, M_O, M_G, DA_QK])]

kernel_name = 'hymba_mlstm_diffattn_macaron'


def rmsnorm(x, g):
    xf = x.astype(jnp.float32)
    y = xf * lax.rsqrt(jnp.mean(xf * xf, axis=-1, keepdims=True) + EPS)
    return (y * g.astype(jnp.float32)).astype(x.dtype)


def swiglu(x, w_gate, w_up, w_down):
    return (jax.nn.silu(x @ w_gate) * (x @ w_up)) @ w_down


def causal_conv(x, w, b):
    c = x.shape[-1]
    y = lax.conv_general_dilated(x, w[:, None, :].astype(x.dtype), window_strides=(1,),
                                 padding=[(CONV_K - 1, 0)],
                                 dimension_numbers=('NWC', 'WIO', 'NWC'),
                                 feature_group_count=c)
    return y + b.astype(x.dtype)


def mlstm(q, k, v, i_pre, f_pre):
    B, S, H, dqk = q.shape
    dv = v.shape[-1]
    nc = S // CHUNK

    def chunk(a):
        return a.astype(jnp.float32).reshape(B, nc, CHUNK, H, -1).transpose(1, 0, 3, 2, 4)

    def chunk_gate(a):
        return a.astype(jnp.float32).reshape(B, nc, CHUNK, H).transpose(1, 0, 3, 2)

    qc = chunk(q)
    kc = chunk(k) * (dqk ** -0.5)
    vc = chunk(v)
    ic = chunk_gate(i_pre)
    lfc = jax.nn.log_sigmoid(chunk_gate(f_pre))
    tri = jnp.tril(jnp.ones((CHUNK, CHUNK), dtype=bool))

    def step(carry, inp):
        C, n, m = carry
        qb, kb, vb, ib, lfb = inp
        b = jnp.cumsum(lfb, axis=-1)
        g = b[..., -1]
        D = jnp.where(tri, b[..., :, None] - b[..., None, :] + ib[..., None, :], -jnp.inf)
        inter = b + m[..., None]
        m_t = jnp.maximum(inter, jnp.max(D, axis=-1))
        w_inter = jnp.exp(inter - m_t)
        P = jnp.exp(D - m_t[..., None]) * jnp.einsum('bhld,bhsd->bhls', qb, kb)
        num = w_inter[..., None] * jnp.einsum('bhld,bhde->bhle', qb, C) + jnp.einsum('bhls,bhse->bhle', P, vb)
        den = w_inter * jnp.einsum('bhld,bhd->bhl', qb, n) + jnp.sum(P, axis=-1)
        h = num / jnp.maximum(jnp.abs(den), jnp.exp(-m_t))[..., None]
        s_w = g[..., None] - b + ib
        m_new = jnp.maximum(g + m, jnp.max(s_w, axis=-1))
        decay = jnp.exp(g + m - m_new)
        w_s = jnp.exp(s_w - m_new[..., None])
        C_new = decay[..., None, None] * C + jnp.einsum('bhs,bhsd,bhse->bhde', w_s, kb, vb)
        n_new = decay[..., None] * n + jnp.einsum('bhs,bhsd->bhd', w_s, kb)
        return (C_new, n_new, m_new), h

    init = (jnp.zeros((B, H, dqk, dv), jnp.float32),
            jnp.zeros((B, H, dqk), jnp.float32),
            jnp.zeros((B, H), jnp.float32))
    _, h = lax.scan(step, init, (qc, kc, vc, ic, lfc))
    return h.transpose(1, 0, 3, 2, 4).reshape(B, S, H, dv)


def t5_bucket(rel):
    n = jnp.maximum(rel, 0)
    nf = jnp.maximum(n, 1).astype(jnp.float32)
    large = MAX_EXACT + (jnp.log(nf / MAX_EXACT) / math.log(MAX_DISTANCE / MAX_EXACT)
                         * (NUM_BUCKETS - MAX_EXACT)).astype(jnp.int32)
    large = jnp.minimum(large, NUM_BUCKETS - 1)
    return jnp.where(n < MAX_EXACT, n, large)


def diff_attention(q, k, v, rel_bias, lam, lam_init, subln_g):
    B, H, _, S, d = q.shape
    scale = d ** -0.5
    outs = []
    for blk in range(S // Q_BLOCK):
        q0 = blk * Q_BLOCK
        q1 = q0 + Q_BLOCK
        qb = q[:, :, :, q0:q1]
        kb = k[:, :, :, :q1]
        vb = v[:, :, :q1]
        rel = (q0 + jnp.arange(Q_BLOCK))[:, None] - jnp.arange(q1)[None, :]
        bias = rel_bias[t5_bucket(rel)].transpose(2, 0, 1).astype(jnp.float32)
        logits = jnp.einsum('bhmqd,bhmkd->bhmqk', qb, kb).astype(jnp.float32) * scale + bias[None, :, None]
        logits = jnp.where(rel >= 0, logits, -jnp.inf)
        a = jax.nn.softmax(logits, axis=-1)
        a = a[:, :, 0] - lam * a[:, :, 1]
        outs.append(jnp.einsum('bhqk,bhkd->bhqd', a.astype(vb.dtype), vb))
    o = jnp.concatenate(outs, axis=2)
    o = rmsnorm(o, subln_g) * (1.0 - lam_init)
    return o.transpose(0, 2, 1, 3).reshape(B, S, H * DV_OF(o))


def DV_OF(o):
    return o.shape[-1]


def cross_attention(u, m, wq, wk, wv, wo):
    B, S, _ = u.shape
    M = m.shape[1]
    q = (u @ wq).reshape(B, S, X_HEADS, X_DH)
    k = (m @ wk).reshape(B, M, X_HEADS, X_DH)
    v = (m @ wv).reshape(B, M, X_HEADS, X_DH)
    logits = jnp.einsum('bshd,bmhd->bhsm', q, k).astype(jnp.float32) * (X_DH ** -0.5)
    a = jax.nn.softmax(logits, axis=-1)
    o = jnp.einsum('bhsm,bmhd->bshd', a.astype(v.dtype), v).reshape(B, S, D_MODEL)
    return o @ wo


def setup_inputs(seed: int = 0) -> dict:
    key = jax.random.key(seed)
    ks = iter(jax.random.split(key, 40))

    def nrm(shape, scale):
        return jax.random.normal(next(ks), shape, jnp.float32) * scale

    def gain(width=D_MODEL):
        return 1.0 + nrm((DEPTH, width), 0.02)

    return {
        'x': nrm((BATCH, SEQ, D_MODEL), 1.0),
        'mem': nrm((BATCH, MEM_LEN, D_MODEL), 1.0),
        'rel_bias': nrm((NUM_BUCKETS, DA_HEADS), 0.5),
        'ffn1_norm_pre': gain(),
        'ffn1_norm_post': gain(),
        'ffn1_w_gate': nrm((DEPTH, D_MODEL, D_FF), D_MODEL ** -0.5),
        'ffn1_w_up': nrm((DEPTH, D_MODEL, D_FF), D_MODEL ** -0.5),
        'ffn1_w_down': nrm((DEPTH, D_FF, D_MODEL), D_FF ** -0.5),
        'mix_norm_pre': gain(),
        'mix_norm_post': gain(),
        'w_in': nrm((DEPTH, D_MODEL, N_IN), D_MODEL ** -0.5),
        'conv_w': nrm((DEPTH, CONV_K, M_QK), CONV_K ** -0.5),
        'conv_b': nrm((DEPTH, M_QK), 0.01),
        'b_igate': nrm((DEPTH, M_HEADS), 0.1),
        'b_fgate': jnp.linspace(3.0, 6.0, M_HEADS, dtype=jnp.float32)[None, :] + nrm((DEPTH, M_HEADS), 0.1),
        'mlstm_norm': gain(M_V),
        'diff_lambda': nrm((DEPTH, 4, DA_D), 0.1),
        'diff_subln': gain(DA_DV),
        'w_out': nrm((DEPTH, D_MIX, D_MODEL), D_MIX ** -0.5),
        'xattn_norm_pre': gain(),
        'xattn_norm_post': gain(),
        'mem_norm': gain(),
        'xattn_wq': nrm((DEPTH, D_MODEL, D_MODEL), D_MODEL ** -0.5),
        'xattn_wk': nrm((DEPTH, D_MODEL, D_MODEL), D_MODEL ** -0.5),
        'xattn_wv': nrm((DEPTH, D_MODEL, D_MODEL), D_MODEL ** -0.5),
        'xattn_wo': nrm((DEPTH, D_MODEL, D_MODEL), D_MODEL ** -0.5),
        'ffn2_norm_pre': gain(),
        'ffn2_norm_post': gain(),
        'ffn2_w_gate': nrm((DEPTH, D_MODEL, D_FF), D_MODEL ** -0.5),
        'ffn2_w_up': nrm((DEPTH, D_MODEL, D_FF), D_MODEL ** -0.5),
        'ffn2_w_down': nrm((DEPTH, D_FF, D_MODEL), D_FF ** -0.5),
    }


def reference(x, mem, rel_bias,
              ffn1_norm_pre, ffn1_norm_post, ffn1_w_gate, ffn1_w_up, ffn1_w_down,
              mix_norm_pre, mix_norm_post, w_in, conv_w, conv_b, b_igate, b_fgate,
              mlstm_norm, diff_lambda, diff_subln, w_out,
              xattn_norm_pre, xattn_norm_post, mem_norm, xattn_wq, xattn_wk, xattn_wv, xattn_wo,
              ffn2_norm_pre, ffn2_norm_post, ffn2_w_gate, ffn2_w_up, ffn2_w_down):
    B, S, _ = x.shape
    for l in range(DEPTH):
        lam_init = 0.8 - 0.6 * math.exp(-0.3 * l)
        f = swiglu(rmsnorm(x, ffn1_norm_pre[l]), ffn1_w_gate[l], ffn1_w_up[l], ffn1_w_down[l])
        x = x + 0.5 * rmsnorm(f, ffn1_norm_post[l])
        u = rmsnorm(x, mix_norm_pre[l])
        z = u @ w_in[l]
        m_qk, m_v, m_o, m_g, d_qk, d_v = jnp.split(z, SPLITS, axis=-1)
        m_qk = jax.nn.silu(causal_conv(m_qk, conv_w[l], conv_b[l]))
        mq, mk = jnp.split(m_qk, 2, axis=-1)
        mq = mq.reshape(B, S, M_HEADS, M_DQK)
        mk = mk.reshape(B, S, M_HEADS, M_DQK)
        mv = m_v.reshape(B, S, M_HEADS, M_DV)
        i_pre = m_g[..., :M_HEADS] + b_igate[l]
        f_pre = m_g[..., M_HEADS:] + b_fgate[l]
        hm = mlstm(mq, mk, mv, i_pre, f_pre).astype(u.dtype)
        hm = rmsnorm(hm, mlstm_norm[l].reshape(M_HEADS, M_DV)).reshape(B, S, M_V)
        y_m = jax.nn.sigmoid(m_o) * hm
        d_qk = d_qk.reshape(B, S, 2, DA_HEADS, 2, DA_D)
        dq = d_qk[:, :, 0].transpose(0, 2, 3, 1, 4)
        dk = d_qk[:, :, 1].transpose(0, 2, 3, 1, 4)
        dv = d_v.reshape(B, S, DA_HEADS, DA_DV).transpose(0, 2, 1, 3)
        lam_p = diff_lambda[l].astype(jnp.float32)
        lam = jnp.exp(jnp.sum(lam_p[0] * lam_p[1])) - jnp.exp(jnp.sum(lam_p[2] * lam_p[3])) + lam_init
        y_d = diff_attention(dq, dk, dv, rel_bias, lam, lam_init, diff_subln[l])
        y = jnp.concatenate([y_m, y_d], axis=-1) @ w_out[l]
        x = x + rmsnorm(y, mix_norm_post[l])
        c = cross_attention(rmsnorm(x, xattn_norm_pre[l]), rmsnorm(mem, mem_norm[l]),
                            xattn_wq[l], xattn_wk[l], xattn_wv[l], xattn_wo[l])
        x = x + rmsnorm(c, xattn_norm_post[l])
        f = swiglu(rmsnorm(x, ffn2_norm_pre[l]), ffn2_w_gate[l], ffn2_w_up[l], ffn2_w_down[l])
        x = x + 0.5 * rmsnorm(f, ffn2_norm_post[l])
    return x
```

```python
import math
from contextlib import ExitStack
import numpy as np
import concourse.bass as bass
import concourse.mybir as mybir
from concourse.bass_utils import run_bass_kernel_spmd

F32 = mybir.dt.float32
BF16 = mybir.dt.bfloat16
AF = mybir.ActivationFunctionType
ALU = mybir.AluOpType

CFG = dict(S=2048, D=2048, DFF=5632, DEPTH=2, MEM=256)
NCORES = 8
EPS = 1e-6
TB = 512
NUM_BUCKETS, MAX_EXACT, MAX_DISTANCE = 32, 16, 128
NEG = -30000.0


def t5_thresholds():
    rel = np.arange(0, 512)
    n = np.maximum(rel, 0)
    nf = np.maximum(n, 1).astype(np.float32)
    large = MAX_EXACT + (np.log(nf / np.float32(MAX_EXACT)).astype(np.float32)
                         / np.float32(math.log(MAX_DISTANCE / MAX_EXACT))
                         * np.float32(NUM_BUCKETS - MAX_EXACT)).astype(np.int32)
    large = np.minimum(large, NUM_BUCKETS - 1)
    bucket = np.where(n < MAX_EXACT, n, large)
    thr = [int(np.argmax(bucket >= b)) for b in range(1, NUM_BUCKETS)]
    return thr, bucket


class Buf:
    __slots__ = ("name", "lastw", "readers", "excl")

    def __init__(self, name, excl=False):
        self.name = name
        self.excl = excl
        self.lastw = None
        self.readers = []


class Prog:
    SAME_ENGINE_SYNC = True

    def __init__(self, nc, es):
        self.nc = nc
        self.es = es
        self.ops = []
        self.engs = {"pe": nc.tensor, "act": nc.scalar, "dve": nc.vector, "pool": nc.gpsimd, "sp": nc.sync}

    def op(self, eng, fn, reads=(), writes=(), kind="c"):
        self.ops.append((eng, fn, tuple(reads), tuple(writes), kind))

    def barrier(self):
        self.ops.append((None, None, (), (), "b"))

    def dma(self, eng, out, in_, reads=(), writes=(), **kw):
        e = self.engs[eng]
        self.op(eng, lambda: e.dma_start(out=out, in_=in_, **kw), reads, writes, kind="d")

    def emit(self):
        nc, es = self.nc, self.es
        ops = self.ops
        n = len(ops)
        deps = [None] * n
        needed = [False] * n
        lastop = {}
        for i, (eng, fn, reads, writes, kind) in enumerate(ops):
            if kind == "b":
                for j in lastop.values():
                    needed[j] = True
                deps[i] = []
                continue
            if kind == "c":
                lastop[eng] = i
            if any(r.excl for r in reads):
                writes = tuple(writes) + tuple(r for r in reads if r.excl)
                reads = tuple(r for r in reads if not r.excl)
            d = set()
            for r in reads:
                if r.lastw is not None:
                    d.add(r.lastw)
            for w in writes:
                if w.lastw is not None:
                    d.add(w.lastw)
                d.update(w.readers)
            for r in reads:
                r.readers.append(i)
            for w in writes:
                w.lastw = i
                w.readers = []
            dd = []
            for j in d:
                je, _, _, _, jk = ops[j]
                if jk == "c" and kind == "c" and je == eng:
                    if eng == "pe" or not self.SAME_ENGINE_SYNC:
                        continue
                dd.append(j)
                needed[j] = True
            deps[i] = dd
        sem_e = {e: es.enter_context(nc.semaphore("s_" + e)) for e in ("pe", "act", "dve", "pool")}
        R = {"sp": 24, "pool": 12, "act": 8}
        rings = {q: [es.enter_context(nc.semaphore("r_%s%d" % (q, k))) for k in range(R[q])] for q in R}
        agsem = es.enter_context(nc.semaphore("agsem"))
        cnt = {e: 0 for e in sem_e}
        dcnt = {q: 0 for q in R}
        agcnt = 0
        token = [None] * n
        waited = {e: {} for e in self.engs}
        final = {}
        for i, (eng, fn, reads, writes, kind) in enumerate(ops):
            if kind == "b":
                for en, EE in self.engs.items():
                    for sn, (s_, v_) in final.items():
                        if waited[en].get(sn, 0) < v_:
                            EE.wait_ge(s_, v_)
                            waited[en][sn] = v_
                continue
            E = self.engs[eng]
            w = {}
            for j in deps[i]:
                s, v = token[j]
                if w.get(s.name, (None, 0))[1] < v:
                    w[s.name] = (s, v)
            if kind == "d":
                k = dcnt[eng]
                rs = rings[eng][k % R[eng]]
                pv = 16 * (k // R[eng])
                if pv > 0 and w.get(rs.name, (None, 0))[1] < pv:
                    w[rs.name] = (rs, pv)
            for sn, (s, v) in w.items():
                if waited[eng].get(sn, 0) < v:
                    E.wait_ge(s, v)
                    waited[eng][sn] = v
            ins = fn()
            if kind == "d":
                k = dcnt[eng]
                rs = rings[eng][k % R[eng]]
                val = 16 * (k // R[eng] + 1)
                ins.then_inc(rs, 16)
                token[i] = (rs, val)
                final[rs.name] = (rs, val)
                dcnt[eng] += 1
            elif kind == "g":
                agcnt += 1
                ins.then_inc(agsem)
                token[i] = (agsem, agcnt)
                final[agsem.name] = (agsem, agcnt)
            else:
                if needed[i]:
                    cnt[eng] += 1
                    ins.then_inc(sem_e[eng], 1)
                    token[i] = (sem_e[eng], cnt[eng])
                    final[sem_e[eng].name] = (sem_e[eng], cnt[eng])
        for sn, (s, v) in final.items():
            nc.sync.wait_ge(s, v)


def build(cfg):
    S, D, DFF, DEPTH, MEM = cfg["S"], cfg["D"], cfg["DFF"], cfg["DEPTH"], cfg["MEM"]
    KC = D // 128
    FC = DFF // 128
    NTB = S // TB
    NT = S // 128
    NIN = 6152
    assert D == 2048
    nc = bass.Bass("TRN2", target_bir_lowering=False)
    es = ExitStack()
    P = Prog(nc, es)

    def dram(name, shape, dt, kind="Internal"):
        return nc.dram_tensor(name, list(shape), dt, kind=kind).ap()

    def sb(name, shape, dt):
        return es.enter_context(nc.sbuf_tensor(name, list(shape), dt))

    xT_in = dram("xT", [D, S], F32, "ExternalInput")
    memT_in = dram("memT", [D, MEM], F32, "ExternalInput")
    relb_in = dram("rel_bias", [NUM_BUCKETS, 8], F32, "ExternalInput")
    outT = dram("outT", [D, S], F32, "ExternalOutput")
    small_names = dict(ffn1_norm_pre=D, ffn1_norm_post=D, mix_norm_pre=D, mix_norm_post=D,
                       xattn_norm_pre=D, xattn_norm_post=D, mem_norm=D, ffn2_norm_pre=D, ffn2_norm_post=D,
                       conv_b=1024, b_igate=4, b_fgate=4, mlstm_norm=1024, diff_subln=128)
    small = {k: dram(k, [DEPTH, v], F32, "ExternalInput") for k, v in small_names.items()}
    convw_in = dram("conv_w", [DEPTH, 4, 1024], F32, "ExternalInput")
    lam_in = dram("diff_lambda", [DEPTH, 4, 64], F32, "ExternalInput")
    wspec = [("ffn1_w_gate", D, DFF, 0), ("ffn1_w_up", D, DFF, 0), ("ffn1_w_down", DFF, D, 1),
             ("w_in", D, NIN, 0), ("w_out", D, D, 0), ("xattn_wq", D, D, 0), ("xattn_wk", D, D, 0),
             ("xattn_wv", D, D, 0), ("xattn_wo", D, D, 0),
             ("ffn2_w_gate", D, DFF, 0), ("ffn2_w_up", D, DFF, 0), ("ffn2_w_down", DFF, D, 1)]
    W = {}
    wbuf = {}
    wjobs = [(l, spec) for l in range(DEPTH) for spec in wspec]

    def setup_weights():
        for (l, (nm, K, N, ax)) in wjobs:
                shp = [K // 8, N] if ax == 0 else [K, N // 8]
                ext = dram("%s_%d" % (nm, l), shp, F32, "ExternalInput")
                shard = dram("%s_%d_bs" % (nm, l), shp, BF16)
                full = dram("%s_%d_bf" % (nm, l), [8 * shp[0], shp[1]], BF16)
                bsh, bfu = Buf(nm + "sh"), Buf(nm + "full")
                nsp = 4 if shp[0] % 4 == 0 else 1
                rr = shp[0] // nsp
                for q in range(nsp):
                    P.dma("pool", shard[q * rr:(q + 1) * rr, :], ext[q * rr:(q + 1) * rr, :], writes=[bsh])
                P.op("pool", (lambda a=shard, b=full: nc.gpsimd.collective_compute(
                    "AllGather", ALU.bypass, replica_groups=[list(range(NCORES))],
                    ins=[a.opt()], outs=[b.opt()])), reads=[bsh], writes=[bfu], kind="g")
                wbuf[(nm, l)] = bfu
                if ax == 0:
                    W[(nm, l)] = full.rearrange("(kc p) n -> p kc n", p=128)
                else:
                    W[(nm, l)] = full.rearrange("(r kc p) n -> p kc r n", r=8, p=128)

    def wview(nm, l, n0, ncols):
        K, N, ax = [(a, b, c) for (x, a, b, c) in wspec if x == nm][0]
        v = W[(nm, l)]
        if ax == 0:
            return v[:, :, n0:n0 + ncols]
        per = N // 8
        r = n0 // per
        assert (n0 % per) + ncols <= per
        return v[:, :, r, (n0 % per):(n0 % per) + ncols]

    dk_ = "ExternalOutput" if cfg.get("debug") else "Internal"
    xres = dram("xres", [D, S], F32)
    zq = dram("zq", [1024, S], BF16, dk_)
    zo = dram("zo", [1024, S], BF16, dk_)
    zd = dram("zd", [2048, S], BF16, dk_)
    zv = dram("zv", [S, 2048], BF16, dk_)
    zg = dram("zg", [8, S], F32, dk_)
    gsc = dram("gsc", [3, 4, S], F32, dk_)
    yT = dram("yT", [D, S], BF16, dk_)
    xb = [Buf("xres%d" % t) for t in range(NTB)]
    zb = Buf("z")
    gb = Buf("gsc")
    yb = Buf("yT")

    ones_bf = sb("ones_bf", [128, 128], BF16)
    ident_bf = sb("ident_bf", [128, 128], BF16)
    mask01 = sb("mask01", [128, 128], BF16)
    gcols = sb("gcols", [128, 9, DEPTH, KC], F32)
    ghalf = sb("ghalf", [128, 2, DEPTH, KC], F32)
    BT = sb("BT", [128, 8, 256], F32)
    relb = sb("relb", [128, NUM_BUCKETS * 8], F32)
    reld = sb("reld", [128, NUM_BUCKETS * 8], F32)
    Rm = sb("Rm", [128, 256], F32)
    Ib = sb("Ib", [128, 256], F32)
    gst = sb("gst", [4, 2, TB], F32)
    cB = Buf("consts")
    thr, _ = t5_thresholds()

    P.op("pool", lambda: nc.gpsimd.memset(ones_bf[:], 1.0), writes=[cB])
    P.op("pool", lambda: nc.gpsimd.memset(ident_bf[:], 0.0), writes=[cB])
    P.op("pool", lambda: nc.gpsimd.affine_select(out=ident_bf[:], in_=ident_bf[:], compare_op=ALU.not_equal,
                                                  fill=1.0, base=0, pattern=[[-1, 128]], channel_multiplier=1),
         writes=[cB])
    P.op("pool", lambda: nc.gpsimd.memset(mask01[:], 1.0), writes=[cB])
    P.op("pool", lambda: nc.gpsimd.affine_select(out=mask01[:], in_=mask01[:], compare_op=ALU.is_ge,
                                                  fill=0.0, base=0, pattern=[[1, 128]], channel_multiplier=-1),
         writes=[cB])
    Rmi = sb("Rmi", [128, 256], mybir.dt.int32)
    P.op("pool", lambda: nc.gpsimd.iota(Rmi[:], pattern=[[1, 256]], base=0, channel_multiplier=-1), writes=[cB])
    P.op("dve", lambda: nc.vector.tensor_copy(out=Rm[:], in_=Rmi[:]), reads=[cB], writes=[cB])
    gnames = ["ffn1_norm_pre", "ffn1_norm_post", "mix_norm_pre", "mix_norm_post", "xattn_norm_pre",
              "xattn_norm_post", "mem_norm", "ffn2_norm_pre", "ffn2_norm_post"]
    for gi, gn in enumerate(gnames):
        for l in range(DEPTH):
            P.dma("sp", gcols[:, gi, l, :], small[gn][l].rearrange("(kc p) -> p kc", p=128), writes=[cB],
                  allow_slow_non_contiguous=True)
    P.dma("sp", relb[:], relb_in.rearrange("b h -> (b h)").partition_broadcast(128), writes=[cB])
    for k2, gi in enumerate((1, 8)):
        P.op("dve", lambda k2=k2, gi=gi: nc.vector.tensor_scalar(
            out=ghalf[:, k2], in0=gcols[:, gi], scalar1=0.5, scalar2=None, op0=ALU.mult), reads=[cB], writes=[cB])
    P.op("dve", lambda: nc.vector.tensor_sub(out=reld[:, 8:], in0=relb[:, 8:], in1=relb[:, :NUM_BUCKETS * 8 - 8]),
         reads=[cB], writes=[cB])
    P.op("dve", lambda: nc.vector.tensor_scalar(out=Ib[:], in0=Rm[:], scalar1=0.0, scalar2=NEG,
                                                 op0=ALU.is_lt, op1=ALU.mult), reads=[cB], writes=[cB])
    for h in range(8):
        P.op("dve", lambda h=h: nc.vector.tensor_scalar(out=BT[:, h], in0=Ib[:], scalar1=relb[:, h:h + 1],
                                                         scalar2=None, op0=ALU.add), reads=[cB], writes=[cB])
    for b in range(1, NUM_BUCKETS):
        P.op("dve", lambda b=b: nc.vector.tensor_scalar(out=Ib[:], in0=Rm[:], scalar1=float(thr[b - 1]),
                                                         scalar2=None, op0=ALU.is_ge), reads=[cB], writes=[cB])
        for h in range(8):
            P.op("dve", lambda b=b, h=h: nc.vector.scalar_tensor_tensor(
                out=BT[:, h], in0=Ib[:], scalar=reld[:, b * 8 + h:b * 8 + h + 1], in1=BT[:, h],
                op0=ALU.mult, op1=ALU.add), reads=[cB], writes=[cB])

    if cfg.get("debug"):
        dbgBT = dram("dbgBT", [128, 8 * 256], F32, "ExternalOutput")
        dbglam = dram("dbglam", [128, 8], F32, "ExternalOutput")
        P.dma("sp", dbgBT, BT[:].rearrange("p h q -> p (h q)"), reads=[cB], writes=[Buf("dbg")])
    setup_weights()

    ps = [es.enter_context(nc.psum_tensor("ps%d" % i, [128, 512], F32)) for i in range(7)]
    psb = [Buf("ps%d" % i, True) for i in range(7)]
    pst = es.enter_context(nc.psum_tensor("pst", [128, 1024], BF16))
    pstb = Buf("pst", True)
    rot = [0]

    def nextps(lo=0, hi=6):
        i = lo + rot[0] % (hi - lo)
        rot[0] += 1
        return i

    NWS = 3
    nws = [NWS]
    ARENA = 79872
    arena = sb("arena", [128, ARENA], BF16)
    xs = arena[:, 0:16384].bitcast(F32).rearrange("p (k t) -> p k t", t=TB)
    fT = arena[:, 16384:32768].bitcast(F32).rearrange("p (k t) -> p k t", t=TB)
    wslot = [arena[:, 32768 + i * 8192:32768 + (i + 1) * 8192] for i in range(NWS)]
    big = arena[:, 57344:ARENA]
    wsb = [Buf("wslot%d" % i) for i in range(NWS)]
    wrot = [0]
    xn = sb("xn", [128, KC, TB], BF16)
    sq = [sb("sq%d" % i, [128, TB], BF16) for i in range(2)]
    rstd = sb("rstd", [128, TB], F32)
    tmpf = [sb("tmpf%d" % i, [128, TB], F32) for i in range(2)]
    xsb, xnb, fTb, rstdb = Buf("xs"), Buf("xn"), Buf("fT"), Buf("rstd")
    sqb = [Buf("sq0"), Buf("sq1")]
    tmpb = [Buf("tmp0"), Buf("tmp1")]
    bigb = {}

    def bb(name):
        if name not in bigb:
            bigb[name] = Buf(name)
        return bigb[name]

    def load_w(nm, l, n0, ncols, kcw):
        i = wrot[0] % nws[0]
        wrot[0] += 1
        view = wslot[i][:, :kcw * ncols].rearrange("p (k n) -> p k n", n=ncols)
        src = wview(nm, l, n0, ncols)
        P.dma("sp", view, src, reads=[wbuf[(nm, l)]], writes=[wsb[i]])
        return view, wsb[i]

    def norm_stats(src_tile, srcb, ncols):
        pi = 6
        for kc in range(KC):
            j = kc % 2
            P.op("act", lambda kc=kc, j=j: nc.scalar.activation(out=sq[j][:, :ncols], in_=src_tile[:, kc, :ncols],
                                                                 func=AF.Square), reads=[srcb], writes=[sqb[j]])
            P.op("pe", lambda kc=kc, j=j: nc.tensor.matmul(ps[pi][:, :ncols], ones_bf[:], sq[j][:, :ncols],
                                                           start=(kc == 0), stop=(kc == KC - 1)),
                 reads=[sqb[j], cB], writes=[psb[pi]])
        P.op("dve", lambda: nc.vector.tensor_scalar(out=rstd[:, :ncols], in0=ps[pi][:, :ncols], scalar1=1.0 / D,
                                                     scalar2=EPS, op0=ALU.mult, op1=ALU.add),
             reads=[psb[pi]], writes=[rstdb])
        P.op("act", lambda: nc.scalar.activation(out=rstd[:, :ncols], in_=rstd[:, :ncols], func=AF.Sqrt),
             reads=[rstdb], writes=[rstdb])
        P.op("dve", lambda: nc.vector.reciprocal(out=rstd[:, :ncols], in_=rstd[:, :ncols]), reads=[rstdb],
             writes=[rstdb])

    def load_norm(src, srcbuf, tb, gi, l):
        P.dma("sp", xs[:], src.rearrange("(kc p) s -> p kc s", p=128)[:, :, tb * TB:(tb + 1) * TB],
              reads=[srcbuf], writes=[xsb])
        norm_stats(xs, xsb, TB)
        for kc in range(KC):
            P.op("dve", lambda kc=kc: nc.vector.scalar_tensor_tensor(
                out=xn[:, kc], in0=xs[:, kc], scalar=gcols[:, gi, l, kc:kc + 1], in1=rstd[:], op0=ALU.mult,
                op1=ALU.mult), reads=[xsb, rstdb, cB], writes=[xnb])

    def proj_fm(nm, l, n0, ntiles, act, actb, kcw, evac, group=None):
        if group is None:
            group = max(1, 8192 // (kcw * 128))
        t = 0
        while t < ntiles:
            g = min(group, ntiles - t)
            view, vb = load_w(nm, l, n0 + t * 128, g * 128, kcw)
            for gg in range(g):
                pi = nextps()
                for kc in range(kcw):
                    P.op("pe", lambda kc=kc, gg=gg, pi=pi, view=view: nc.tensor.matmul(
                        ps[pi][:, :act.shape[-1]], view[:, kc, gg * 128:(gg + 1) * 128], act[:, kc],
                        start=(kc == 0), stop=(kc == kcw - 1)), reads=[vb, actb], writes=[psb[pi]])
                evac(t + gg, pi)
            t += g

    def post_residual(gsel, l, half, dst, dstbuf, tb):
        norm_stats(fT, fTb, TB)
        for kc in range(KC):
            j = kc % 2
            gc = ghalf[:, gsel, l, kc:kc + 1] if half else gcols[:, gsel, l, kc:kc + 1]
            P.op("dve", lambda kc=kc, j=j, gc=gc: nc.vector.scalar_tensor_tensor(
                out=tmpf[j][:], in0=fT[:, kc], scalar=gc, in1=rstd[:], op0=ALU.mult, op1=ALU.mult),
                reads=[fTb, rstdb, cB], writes=[tmpb[j]])
            P.op("dve", lambda kc=kc, j=j: nc.vector.tensor_tensor(out=xs[:, kc], in0=xs[:, kc], in1=tmpf[j][:],
                                                                   op=ALU.add), reads=[tmpb[j], xsb], writes=[xsb])
        P.dma("sp", dst.rearrange("(kc p) s -> p kc s", p=128)[:, :, tb * TB:(tb + 1) * TB], xs[:],
              reads=[xsb], writes=[dstbuf])

    def evac_fT(t, pi):
        P.op("act", lambda: nc.scalar.copy(out=fT[:, t], in_=ps[pi][:]), reads=[psb[pi]], writes=[fTb])

    def ffn(l, pre, src, srcbufs, dst, dstbufs, gi_pre, gsel_post):
        P.barrier()
        hT = big[:, :FC * TB].rearrange("p (f t) -> p f t", t=TB)
        hb = bb("hT")
        sg = [sb("sg%d_%s%d" % (i, pre, l), [128, TB], F32) for i in range(2)] if False else None
        for tb in range(NTB):
            load_norm(src, srcbufs[tb], tb, gi_pre, l)
            t = 0
            while t < FC:
                g = min(4, FC - t)
                vg, vgb = load_w(pre + "_w_gate", l, t * 128, g * 128, KC)
                vu, vub = load_w(pre + "_w_up", l, t * 128, g * 128, KC)
                for gg in range(g):
                    pg, pu = nextps(), nextps()
                    for (view, vb, pi) in ((vg, vgb, pg), (vu, vub, pu)):
                        for kc in range(KC):
                            P.op("pe", lambda kc=kc, gg=gg, pi=pi, view=view: nc.tensor.matmul(
                                ps[pi][:], view[:, kc, gg * 128:(gg + 1) * 128], xn[:, kc],
                                start=(kc == 0), stop=(kc == KC - 1)), reads=[vb, xnb], writes=[psb[pi]])
                    j = (t + gg) % 2
                    P.op("act", lambda pg=pg, j=j: nc.scalar.activation(out=tmpf[j][:], in_=ps[pg][:], func=AF.Silu),
                         reads=[psb[pg]], writes=[tmpb[j]])
                    P.op("dve", lambda pu=pu, j=j, f=t + gg: nc.vector.tensor_tensor(
                        out=hT[:, f], in0=tmpf[j][:], in1=ps[pu][:], op=ALU.mult),
                        reads=[tmpb[j], psb[pu]], writes=[hb])
                t += g
            proj_fm(pre + "_w_down", l, 0, KC, hT, hb, FC, evac_fT, group=1)
            post_residual(gsel_post, l, True, dst, dstbufs[tb], tb)

    def mix_in(l, src, srcbufs):
        P.barrier()
        stage = big[:, :8 * TB].rearrange("p (a t) -> p a t", t=TB)
        stb = [bb("stg%d" % i) for i in range(8)]
        gstb = Buf("gst")
        gbias = sb("gbias%d" % l, [4, 2], F32)
        P.dma("sp", gbias[:, 0:1], small["b_igate"][l].rearrange("(h o) -> h o", o=1), writes=[gstb])
        P.dma("sp", gbias[:, 1:2], small["b_fgate"][l].rearrange("(h o) -> h o", o=1), writes=[gstb])
        srot = [0]
        for tb in range(NTB):
            load_norm(src, srcbufs[tb], tb, 2, l)
            cs = slice(tb * TB, (tb + 1) * TB)

            def fm_block(n0, ntiles, dstT, func):
                def ev(t, pi):
                    si = srot[0] % 8
                    srot[0] += 1
                    if func is None:
                        P.op("act", lambda: nc.scalar.copy(out=stage[:, si], in_=ps[pi][:]),
                             reads=[psb[pi]], writes=[stb[si]])
                    else:
                        P.op("act", lambda: nc.scalar.activation(out=stage[:, si], in_=ps[pi][:], func=func),
                             reads=[psb[pi]], writes=[stb[si]])
                    P.dma("sp", dstT[t * 128:(t + 1) * 128, cs], stage[:, si], reads=[stb[si]], writes=[zb])
                proj_fm("w_in", l, n0, ntiles, xn, xnb, KC, ev)
            fm_block(0, 8, zq, None)
            fm_block(2048, 8, zo, AF.Sigmoid)
            fm_block(3080, 16, zd, None)
            view, vb = load_w("w_in", l, 3072, 8, KC)
            for gsel in range(2):
                pi = nextps()
                for kc in range(KC):
                    P.op("pe", lambda kc=kc, pi=pi, gsel=gsel, view=view: nc.tensor.matmul(
                        ps[pi][:4, :], view[:, kc, gsel * 4:gsel * 4 + 4], xn[:, kc], start=(kc == 0),
                        stop=(kc == KC - 1)), reads=[vb, xnb], writes=[psb[pi]])
                P.op("dve", lambda pi=pi, gsel=gsel: nc.vector.tensor_scalar(
                    out=gst[:, gsel], in0=ps[pi][:4, :], scalar1=gbias[:, gsel:gsel + 1], scalar2=None, op0=ALU.add),
                    reads=[psb[pi], gstb], writes=[gstb])
            P.dma("sp", zg.rearrange("(g h) s -> h g s", g=2)[:, :, cs], gst[:], reads=[gstb], writes=[zb])
            for vi, n0 in enumerate((1024, 1536, 5128, 5640)):
                view, vb = load_w("w_in", l, n0, 512, KC)
                for tt in range(TB // 128):
                    pi = nextps()
                    for kc in range(KC):
                        P.op("pe", lambda kc=kc, pi=pi, tt=tt, view=view: nc.tensor.matmul(
                            ps[pi][:], xn[:, kc, tt * 128:(tt + 1) * 128], view[:, kc, :], start=(kc == 0),
                            stop=(kc == KC - 1)), reads=[vb, xnb], writes=[psb[pi]])
                    si = srot[0] % 8
                    srot[0] += 1
                    P.op("act", lambda pi=pi, si=si: nc.scalar.copy(out=stage[:, si], in_=ps[pi][:]),
                         reads=[psb[pi]], writes=[stb[si]])
                    r0 = tb * TB + tt * 128
                    P.dma("sp", zv[r0:r0 + 128, vi * 512:(vi + 1) * 512], stage[:, si], reads=[stb[si]], writes=[zb])

    def mlstm(l):
        P.barrier()
        o = 0

        def carve(nel, dt=BF16, parts=128):
            nonlocal o
            a = arena[:parts, o:o + nel] if dt == BF16 else arena[:parts, o:o + 2 * nel].bitcast(F32)
            o += nel if dt == BF16 else 2 * nel
            return a
        g4 = carve(5 * S, F32, 4).rearrange("p (a s) -> p a s", s=S)
        g4b = Buf("g4")
        eb4 = carve(S, F32, 4)
        ones4 = carve(S, F32, 4)
        P.op("dve", lambda: nc.vector.memset(ones4, 1.0), writes=[g4b])
        P.dma("sp", g4[:, 0:2], zg.rearrange("(g h) s -> h g s", g=2), reads=[zb], writes=[g4b])
        P.op("act", lambda: nc.scalar.activation(out=g4[:, 1], in_=g4[:, 1], func=AF.Exp, scale=-1.0),
             reads=[g4b], writes=[g4b])
        P.op("act", lambda: nc.scalar.activation(out=g4[:, 1], in_=g4[:, 1], func=AF.Ln, bias=1.0),
             reads=[g4b], writes=[g4b])
        P.op("dve", lambda: nc.vector.tensor_tensor_scan(out=g4[:, 2], data0=ones4, data1=g4[:, 1], initial=0.0,
                                                          op0=ALU.mult, op1=ALU.add), reads=[g4b, cB], writes=[g4b])
        P.op("dve", lambda: nc.vector.tensor_add(out=g4[:, 3], in0=g4[:, 0], in1=g4[:, 2]), reads=[g4b], writes=[g4b])
        P.op("dve", lambda: nc.vector.tensor_tensor_scan(out=g4[:, 4], data0=g4[:, 3], data1=g4[:, 3], initial=0.0,
                                                          op0=ALU.max, op1=ALU.max), reads=[g4b], writes=[g4b])
        P.op("dve", lambda: nc.vector.tensor_sub(out=eb4, in0=g4[:, 2], in1=g4[:, 4]), reads=[g4b], writes=[g4b])
        P.op("act", lambda: nc.scalar.activation(out=eb4, in_=eb4, func=AF.Exp), reads=[g4b], writes=[g4b])
        P.dma("sp", gsc[0], g4[:, 3], reads=[g4b], writes=[gb])
        P.dma("sp", gsc[1], g4[:, 4], reads=[g4b], writes=[gb])
        P.dma("sp", gsc[2], eb4, reads=[g4b], writes=[gb])
        cw = sb("cw%d" % l, [128, 8, 4], F32)
        cbias = sb("cb%d" % l, [128, 8], F32)
        gm = sb("gm%d" % l, [128, 8], F32)
        pb = Buf("mparams")
        for j in range(4):
            P.dma("sp", cw[:, :, j], convw_in[l, j].rearrange("(t p) -> p t", p=128), writes=[pb],
                  allow_slow_non_contiguous=True)
        P.dma("sp", cbias[:], small["conv_b"][l].rearrange("(t p) -> p t", p=128), writes=[pb],
              allow_slow_non_contiguous=True)
        P.dma("sp", gm[:], small["mlstm_norm"][l].rearrange("(t p) -> p t", p=128), writes=[pb],
              allow_slow_non_contiguous=True)
        qT = carve(S)
        kT = carve(S)
        zin = carve(S)
        acc = carve(S, F32)
        vaug = carve(NT * 257).rearrange("p (j c) -> p j c", c=257)
        Mgrep = carve(S, F32)
        sig = [carve(S), carve(S)]
        yst = [carve(S), carve(S)]
        E = [carve(TB, F32), carve(TB, F32)]
        Wt = [carve(TB), carve(TB)]
        hn = carve(256)
        junk = carve(256, F32)
        assert o <= ARENA, o
        ucol = sb("ucol%d" % l, [128, NT], F32)
        ebcol = sb("ebcol%d" % l, [128, NT], F32)
        sm = sb("sm%d" % l, [128, 8], F32)
        qTb, kTb, zinb, accb, vab, mgb, colb, smb = (Buf(x) for x in "qT kT zin acc vaug Mgrep cols sm".split())
        sigb = [Buf("sig0"), Buf("sig1")]
        ystb = [Buf("yst0"), Buf("yst1")]
        Eb = [Buf("E0"), Buf("E1")]
        Wb = [Buf("W0"), Buf("W1")]
        hnb, junkb = Buf("hn"), Buf("junk")
        scale = 128 ** -0.5
        for h in range(4):
            P.dma("sp", ucol[:], gsc[0, h].rearrange("(j p) -> p j", p=128), reads=[gb], writes=[colb],
                  allow_slow_non_contiguous=True)
            P.dma("sp", ebcol[:], gsc[2, h].rearrange("(j p) -> p j", p=128), reads=[gb], writes=[colb],
                  allow_slow_non_contiguous=True)
            P.dma("sp", Mgrep, gsc[1, h].partition_broadcast(128), reads=[gb], writes=[mgb])
            for (dstT, dstTb, t8) in ((qT, qTb, h), (kT, kTb, 4 + h)):
                P.dma("sp", zin, zq[t8 * 128:(t8 + 1) * 128, :], reads=[zb], writes=[zinb])
                P.op("dve", lambda t8=t8: nc.vector.tensor_scalar(out=acc, in0=zin, scalar1=cw[:, t8, 3:4],
                                                                   scalar2=None, op0=ALU.mult),
                     reads=[zinb, pb], writes=[accb])
                for sft in (1, 2, 3):
                    P.op("dve", lambda t8=t8, sft=sft: nc.vector.scalar_tensor_tensor(
                        out=acc[:, sft:], in0=zin[:, :S - sft], scalar=cw[:, t8, 3 - sft:4 - sft], in1=acc[:, sft:],
                        op0=ALU.mult, op1=ALU.add), reads=[zinb, pb, accb], writes=[accb])
                P.op("act", lambda t8=t8, dstT=dstT: nc.scalar.activation(out=dstT, in_=acc, func=AF.Silu,
                                                                            bias=cbias[:, t8:t8 + 1]),
                     reads=[accb, pb], writes=[dstTb])
            P.dma("sp", vaug[:, :, :256], zv[:, h * 256:(h + 1) * 256].rearrange("(j p) c -> p j c", p=128),
                  reads=[zb], writes=[vab])
            P.op("dve", lambda: nc.vector.memset(vaug[:, :, 256:257], 1.0), writes=[vab])
            for dvt in range(2):
                P.dma("sp", sig[dvt], zo[(h * 2 + dvt) * 128:(h * 2 + dvt + 1) * 128, :], reads=[zb], writes=[sigb[dvt]])
            erot = 0
            for qb in range(NTB):
                accs = [2, 3, 4, 5]
                for j in range(4 * qb + 4):
                    r0 = max(0, j - 4 * qb)
                    c0 = r0 * 128
                    n = TB - c0
                    q0 = qb * TB + c0
                    pi = erot % 2
                    e = erot % 2
                    erot += 1
                    P.op("pe", lambda j=j, pi=pi, q0=q0, n=n: nc.tensor.matmul(
                        ps[pi][:, :n], kT[:, j * 128:(j + 1) * 128], qT[:, q0:q0 + n], start=True, stop=True),
                        reads=[kTb, qTb], writes=[psb[pi]])
                    if j < 4 * qb:
                        P.op("act", lambda j=j, e=e, q0=q0, n=n: nc.scalar.activation(
                            out=E[e][:, :n], in_=Mgrep[:, q0:q0 + n], func=AF.Exp, scale=-1.0, bias=ucol[:, j:j + 1]),
                            reads=[mgb, colb], writes=[Eb[e]])
                    else:
                        P.op("dve", lambda j=j, e=e, q0=q0, n=n: nc.vector.tensor_scalar(
                            out=E[e][:, :n], in0=Mgrep[:, q0:q0 + n], scalar1=ucol[:, j:j + 1], scalar2=0.0,
                            op0=ALU.subtract, op1=ALU.max), reads=[mgb, colb], writes=[Eb[e]])
                        P.op("act", lambda e=e, n=n: nc.scalar.activation(out=E[e][:, :n], in_=E[e][:, :n],
                                                                           func=AF.Exp, scale=-1.0),
                             reads=[Eb[e]], writes=[Eb[e]])
                    P.op("dve", lambda e=e, pi=pi, n=n: nc.vector.scalar_tensor_tensor(
                        out=Wt[e][:, :n], in0=ps[pi][:, :n], scalar=scale, in1=E[e][:, :n], op0=ALU.mult,
                        op1=ALU.mult), reads=[psb[pi], Eb[e]], writes=[Wb[e]])
                    if j >= 4 * qb:
                        P.op("dve", lambda e=e: nc.vector.tensor_tensor(out=Wt[e][:, :128], in0=Wt[e][:, :128],
                                                                         in1=mask01[:], op=ALU.mult),
                             reads=[Wb[e], cB], writes=[Wb[e]])
                    for r in range(r0, 4):
                        P.op("pe", lambda e=e, r=r, r0=r0, j=j, qb=qb: nc.tensor.matmul(
                            ps[accs[r]][:, :257], Wt[e][:, (r - r0) * 128:(r - r0 + 1) * 128], vaug[:, j, :],
                            start=(j == 0), stop=(j == 4 * qb + r)), reads=[Wb[e], vab], writes=[psb[accs[r]]])
                for r in range(4):
                    i = 4 * qb + r
                    a = accs[r]
                    A = ps[a]
                    ab = psb[a]
                    P.op("dve", lambda A=A: nc.vector.tensor_scalar(out=sm[:, 7:8], in0=A[:, 256:257], scalar1=-1.0,
                                                                     scalar2=None, op0=ALU.mult),
                         reads=[ab], writes=[smb])
                    P.op("dve", lambda A=A: nc.vector.tensor_tensor(out=sm[:, 7:8], in0=sm[:, 7:8], in1=A[:, 256:257],
                                                                     op=ALU.max), reads=[ab, smb], writes=[smb])
                    P.op("dve", lambda i=i: nc.vector.tensor_scalar(out=sm[:, 0:1], in0=sm[:, 7:8],
                                                                     scalar1=ebcol[:, i:i + 1], scalar2=None,
                                                                     op0=ALU.max),
                         reads=[smb, colb], writes=[smb])
                    P.op("dve", lambda: nc.vector.reciprocal(out=sm[:, 1:2], in_=sm[:, 0:1]), reads=[smb], writes=[smb])
                    P.op("dve", lambda: nc.vector.memset(sm[:, 2:3], 0.0), writes=[smb])
                    P.op("act", lambda A=A: nc.scalar.activation(out=junk, in_=A[:, :256], func=AF.Square,
                                                                 accum_out=sm[:, 2:3]),
                         reads=[ab], writes=[smb, junkb])
                    P.op("dve", lambda: nc.vector.tensor_tensor(out=sm[:, 3:4], in0=sm[:, 1:2], in1=sm[:, 1:2],
                                                                 op=ALU.mult), reads=[smb], writes=[smb])
                    P.op("dve", lambda: nc.vector.scalar_tensor_tensor(out=sm[:, 4:5], in0=sm[:, 2:3],
                                                                        scalar=1.0 / 256, in1=sm[:, 3:4],
                                                                        op0=ALU.mult, op1=ALU.mult),
                         reads=[smb], writes=[smb])
                    P.op("dve", lambda: nc.vector.tensor_scalar(out=sm[:, 5:6], in0=sm[:, 4:5], scalar1=EPS,
                                                                 scalar2=None, op0=ALU.add),
                         reads=[smb], writes=[smb])
                    P.op("act", lambda: nc.scalar.activation(out=sm[:, 5:6], in_=sm[:, 5:6], func=AF.Sqrt),
                         reads=[smb], writes=[smb])
                    P.op("dve", lambda: nc.vector.reciprocal(out=sm[:, 5:6], in_=sm[:, 5:6]), reads=[smb],
                         writes=[smb])
                    P.op("dve", lambda: nc.vector.tensor_tensor(out=sm[:, 6:7], in0=sm[:, 5:6], in1=sm[:, 1:2],
                                                                 op=ALU.mult), reads=[smb], writes=[smb])
                    P.op("act", lambda A=A: nc.scalar.activation(out=hn, in_=A[:, :256], func=AF.Copy,
                                                                 scale=sm[:, 6:7]), reads=[ab, smb], writes=[hnb])
                    for dvt in range(2):
                        P.op("pe", lambda dvt=dvt: nc.tensor.transpose(pst[:, dvt * 128:(dvt + 1) * 128],
                                                                        hn[:, dvt * 128:(dvt + 1) * 128], ident_bf[:]),
                             reads=[hnb, cB], writes=[pstb])
                    for dvt in range(2):
                        P.op("dve", lambda dvt=dvt, i=i, h=h: nc.vector.scalar_tensor_tensor(
                            out=yst[dvt][:, i * 128:(i + 1) * 128], in0=pst[:, dvt * 128:(dvt + 1) * 128],
                            scalar=gm[:, h * 2 + dvt:h * 2 + dvt + 1], in1=sig[dvt][:, i * 128:(i + 1) * 128],
                            op0=ALU.mult, op1=ALU.mult), reads=[pstb, pb, sigb[dvt]], writes=[ystb[dvt]])
            for dvt in range(2):
                P.dma("sp", yT[(h * 2 + dvt) * 128:(h * 2 + dvt + 1) * 128, :], yst[dvt], reads=[ystb[dvt]],
                      writes=[yb])

    def diffattn(l):
        P.barrier()
        lam_init = 0.8 - 0.6 * math.exp(-0.3 * l)
        lamt = sb("lamt%d" % l, [128, 256], F32)
        lams = sb("lams%d" % l, [128, 8], F32)
        gsub = sb("gsub%d" % l, [128, 128], F32)
        lb = Buf("lam")
        P.dma("sp", lamt[:], lam_in[l].rearrange("a d -> (a d)").partition_broadcast(128), writes=[lb])
        lam4 = lamt[:].rearrange("p (a d) -> p a d", d=64)
        P.dma("sp", gsub[:], small["diff_subln"][l].partition_broadcast(128), writes=[lb])
        P.op("dve", lambda: nc.vector.tensor_scalar(out=gsub[:], in0=gsub[:], scalar1=1.0 - lam_init, scalar2=None,
                                                     op0=ALU.mult), reads=[lb], writes=[lb])
        for a in range(2):
            P.op("dve", lambda a=a: nc.vector.tensor_tensor(out=lam4[:, 2 * a], in0=lam4[:, 2 * a],
                                                             in1=lam4[:, 2 * a + 1], op=ALU.mult),
                 reads=[lb], writes=[lb])
            P.op("dve", lambda a=a: nc.vector.reduce_sum(out=lams[:, a:a + 1], in_=lam4[:, 2 * a],
                                                          axis=mybir.AxisListType.X), reads=[lb], writes=[lb])
            P.op("act", lambda a=a: nc.scalar.activation(out=lams[:, 2 + a:3 + a], in_=lams[:, a:a + 1], func=AF.Exp),
                 reads=[lb], writes=[lb])
        P.op("dve", lambda: nc.vector.tensor_sub(out=lams[:, 4:5], in0=lams[:, 3:4], in1=lams[:, 2:3]),
             reads=[lb], writes=[lb])
        P.op("dve", lambda: nc.vector.tensor_scalar(out=lams[:, 5:6], in0=lams[:, 4:5], scalar1=-lam_init,
                                                     scalar2=None, op0=ALU.add), reads=[lb], writes=[lb])
        if cfg.get("debug"):
            P.dma("sp", dbglam, lams[:], reads=[lb], writes=[Buf("dbg2")])
        o = 0

        def carve(nel, dt=BF16):
            nonlocal o
            a = arena[:, o:o + nel] if dt == BF16 else arena[:, o:o + 2 * nel].bitcast(F32)
            o += nel if dt == BF16 else 2 * nel
            return a
        dqm = [carve(S)[0:64, :], carve(S)[0:64, :]]
        dkm = [carve(S)[0:64, :], carve(S)[0:64, :]]
        dv = carve(NT * 129).rearrange("p (j c) -> p j c", c=129)
        yst = carve(S)
        Ef = [carve(TB), carve(TB)]
        tb_ = [carve(256, F32), carve(256, F32)]
        t1 = carve(128, F32)
        of = carve(128, F32)
        onb_ = carve(128)
        junk = carve(128, F32)
        sm = sb("dsm%d" % l, [128, 8], F32)
        dqb, dkb, dvb, ystb, t1b, ofb, onbb, junkb, smb = (Buf(x) for x in "dq dk dv yst t1 of on junk sm".split())
        Efb = [Buf("Ef0"), Buf("Ef1")]
        tbb = [Buf("tb0"), Buf("tb1")]
        scale = 64 ** -0.5
        erot = 0
        for h in range(8):
            for m_ in range(2):
                P.dma("sp", dqm[m_], zd[h * 128 + m_ * 64:h * 128 + (m_ + 1) * 64, :], reads=[zb], writes=[dqb])
                P.dma("sp", dkm[m_], zd[1024 + h * 128 + m_ * 64:1024 + h * 128 + (m_ + 1) * 64, :], reads=[zb],
                      writes=[dkb])
            P.dma("sp", dv[:, :, :128], zv[:, 1024 + h * 128:1024 + (h + 1) * 128].rearrange("(j p) c -> p j c", p=128),
                  reads=[zb], writes=[dvb])
            P.op("dve", lambda: nc.vector.memset(dv[:, :, 128:129], 1.0), writes=[dvb])
            cfar = relb[:, 31 * 8 + h:31 * 8 + h + 1]
            for i in range(NT):
                oacc = [2 + 2 * (i % 2), 3 + 2 * (i % 2)]
                for m in range(2):
                    pr = m
                    oa = oacc[m]
                    js = list(range(0, max(0, i - 1)))
                    first = [True]

                    def av(j, lhs, lhsb):
                        P.op("pe", lambda j=j, lhs=lhs, st=first[0], oa=oa, i=i: nc.tensor.matmul(
                            ps[oa][:, :129], lhs, dv[:, j, :], start=st, stop=(j == i)),
                            reads=[lhsb, dvb], writes=[psb[oa]])
                        first[0] = False
                    for g0 in range(0, len(js), 4):
                        grp = js[g0:g0 + 4]
                        pi = erot % 2
                        e = erot % 2
                        erot += 1
                        for jj, j in enumerate(grp):
                            P.op("pe", lambda jj=jj, j=j, pi=pi, pr=pr, i=i: nc.tensor.matmul(
                                ps[pi][:, jj * 128:(jj + 1) * 128], dkm[pr][:, j * 128:(j + 1) * 128],
                                dqm[pr][:, i * 128:(i + 1) * 128], start=True, stop=True),
                                reads=[dkb, dqb], writes=[psb[pi]])
                        n = len(grp) * 128
                        P.op("act", lambda pi=pi, e=e, n=n, cfar=cfar: nc.scalar.activation(
                            out=Ef[e][:, :n], in_=ps[pi][:, :n], func=AF.Exp, scale=scale, bias=cfar),
                            reads=[psb[pi], cB], writes=[Efb[e]])
                        for jj, j in enumerate(grp):
                            av(j, Ef[e][:, jj * 128:(jj + 1) * 128], Efb[e])
                    band = [i] + ([i - 1] if i >= 1 else [])
                    pi = erot % 2
                    e = erot % 2
                    erot += 1
                    for jj, j in enumerate(band):
                        P.op("pe", lambda jj=jj, j=j, pi=pi, pr=pr, i=i: nc.tensor.matmul(
                            ps[pi][:, jj * 128:(jj + 1) * 128], dkm[pr][:, j * 128:(j + 1) * 128],
                            dqm[pr][:, i * 128:(i + 1) * 128], start=True, stop=True),
                            reads=[dkb, dqb], writes=[psb[pi]])
                    n = len(band) * 128
                    P.op("dve", lambda pi=pi, e=e, n=n, h=h: nc.vector.scalar_tensor_tensor(
                        out=tb_[e][:, :n], in0=ps[pi][:, :n], scalar=scale, in1=BT[:, h, :n], op0=ALU.mult,
                        op1=ALU.add), reads=[psb[pi], cB], writes=[tbb[e]])
                    P.op("act", lambda e=e, n=n: nc.scalar.activation(out=Ef[e][:, :n], in_=tb_[e][:, :n], func=AF.Exp),
                         reads=[tbb[e]], writes=[Efb[e]])
                    if i >= 1:
                        av(i - 1, Ef[e][:, 128:256], Efb[e])
                    av(i, Ef[e][:, 0:128], Efb[e])
                o1, o2 = ps[oacc[0]], ps[oacc[1]]
                o1b, o2b = psb[oacc[0]], psb[oacc[1]]
                P.op("dve", lambda o1=o1: nc.vector.reciprocal(out=sm[:, 0:1], in_=o1[:, 128:129]),
                     reads=[o1b], writes=[smb])
                P.op("dve", lambda o2=o2: nc.vector.reciprocal(out=sm[:, 1:2], in_=o2[:, 128:129]),
                     reads=[o2b], writes=[smb])
                P.op("dve", lambda: nc.vector.tensor_tensor(out=sm[:, 2:3], in0=sm[:, 1:2], in1=lams[:, 5:6],
                                                             op=ALU.mult), reads=[smb, lb], writes=[smb])
                P.op("act", lambda o1=o1: nc.scalar.activation(out=t1, in_=o1[:, :128], func=AF.Copy, scale=sm[:, 0:1]),
                     reads=[o1b, smb], writes=[t1b])
                P.op("dve", lambda o2=o2: nc.vector.scalar_tensor_tensor(out=of, in0=o2[:, :128], scalar=sm[:, 2:3],
                                                                          in1=t1, op0=ALU.mult, op1=ALU.add),
                     reads=[o2b, smb, t1b], writes=[ofb])
                P.op("dve", lambda: nc.vector.memset(sm[:, 3:4], 0.0), writes=[smb])
                P.op("act", lambda: nc.scalar.activation(out=junk, in_=of, func=AF.Square, accum_out=sm[:, 3:4]),
                     reads=[ofb], writes=[smb, junkb])
                P.op("dve", lambda: nc.vector.tensor_scalar(out=sm[:, 4:5], in0=sm[:, 3:4], scalar1=1.0 / 128,
                                                             scalar2=EPS, op0=ALU.mult, op1=ALU.add),
                     reads=[smb], writes=[smb])
                P.op("act", lambda: nc.scalar.activation(out=sm[:, 5:6], in_=sm[:, 4:5], func=AF.Sqrt),
                     reads=[smb], writes=[smb])
                P.op("dve", lambda: nc.vector.reciprocal(out=sm[:, 5:6], in_=sm[:, 5:6]), reads=[smb], writes=[smb])
                P.op("dve", lambda: nc.vector.scalar_tensor_tensor(out=onb_, in0=of, scalar=sm[:, 5:6], in1=gsub[:],
                                                                    op0=ALU.mult, op1=ALU.mult),
                     reads=[ofb, smb, lb], writes=[onbb])
                P.op("pe", lambda: nc.tensor.transpose(pst[:, 512:640], onb_, ident_bf[:]), reads=[onbb, cB],
                     writes=[pstb])
                P.op("act", lambda i=i: nc.scalar.copy(out=yst[:, i * 128:(i + 1) * 128], in_=pst[:, 512:640]),
                     reads=[pstb], writes=[ystb])
            P.dma("sp", yT[1024 + h * 128:1024 + (h + 1) * 128, :], yst, reads=[ystb], writes=[yb])

    def mix_out(l):
        P.barrier()
        for tb in range(NTB):
            P.dma("sp", xs[:], xres.rearrange("(kc p) s -> p kc s", p=128)[:, :, tb * TB:(tb + 1) * TB],
                  reads=[xb[tb]], writes=[xsb])
            P.dma("sp", xn[:], yT.rearrange("(kc p) s -> p kc s", p=128)[:, :, tb * TB:(tb + 1) * TB],
                  reads=[yb], writes=[xnb])
            proj_fm("w_out", l, 0, KC, xn, xnb, KC, evac_fT)
            post_residual(3, l, False, xres, xb[tb], tb)

    def xattn(l):
        P.barrier()
        nws[0] = 2
        o = 49152

        def carve(nel, dt=BF16):
            nonlocal o
            a = arena[:, o:o + nel] if dt == BF16 else arena[:, o:o + 2 * nel].bitcast(F32)
            o += nel if dt == BF16 else 2 * nel
            return a
        MT = MEM // 128
        kx = carve(KC * MEM).rearrange("p (k m) -> p k m", m=MEM)
        vx = carve(MT * D).rearrange("p (t d) -> p t d", d=D)
        qx = carve(KC * TB).rearrange("p (k t) -> p k t", t=TB)
        ox = carve(KC * TB).rearrange("p (k t) -> p k t", t=TB)
        Ex = carve(MT * TB).rearrange("p (t s) -> p t s", s=TB)
        rden = carve(TB, F32)
        assert o <= ARENA, o
        kxb, vxb, qxb, oxb, Exb, rdb = (Buf(x) for x in "kx vx qx ox Ex rden".split())
        P.dma("sp", xs[:, :, :MEM], memT_in.rearrange("(kc p) s -> p kc s", p=128), writes=[xsb])
        norm_stats(xs, xsb, MEM)
        for kc in range(KC):
            P.op("dve", lambda kc=kc: nc.vector.scalar_tensor_tensor(
                out=xn[:, kc, :MEM], in0=xs[:, kc, :MEM], scalar=gcols[:, 6, l, kc:kc + 1], in1=rstd[:, :MEM],
                op0=ALU.mult, op1=ALU.mult), reads=[xsb, rstdb, cB], writes=[xnb])
        mn = xn[:, :, :MEM]

        def ev_k(t, pi):
            P.op("act", lambda: nc.scalar.copy(out=kx[:, t], in_=ps[pi][:, :MEM]), reads=[psb[pi]], writes=[kxb])
        proj_fm("xattn_wk", l, 0, KC, mn, xnb, KC, ev_k)
        for cg in range(D // 512):
            view, vb = load_w("xattn_wv", l, cg * 512, 512, KC)
            for mt in range(MT):
                pi = nextps()
                for kc in range(KC):
                    P.op("pe", lambda kc=kc, pi=pi, mt=mt, view=view: nc.tensor.matmul(
                        ps[pi][:], xn[:, kc, mt * 128:(mt + 1) * 128], view[:, kc, :], start=(kc == 0),
                        stop=(kc == KC - 1)), reads=[vb, xnb], writes=[psb[pi]])
                P.op("act", lambda pi=pi, mt=mt, cg=cg: nc.scalar.copy(out=vx[:, mt, cg * 512:(cg + 1) * 512],
                                                                       in_=ps[pi][:]), reads=[psb[pi]], writes=[vxb])
        scale = 512 ** -0.5
        for tb in range(NTB):
            load_norm(xres, xb[tb], tb, 4, l)

            def ev_q(t, pi):
                P.op("act", lambda: nc.scalar.copy(out=qx[:, t], in_=ps[pi][:]), reads=[psb[pi]], writes=[qxb])
            proj_fm("xattn_wq", l, 0, KC, xn, xnb, KC, ev_q)
            for h in range(4):
                for mt in range(MT):
                    pi = nextps()
                    for cc in range(4):
                        P.op("pe", lambda cc=cc, pi=pi, mt=mt, h=h: nc.tensor.matmul(
                            ps[pi][:], kx[:, h * 4 + cc, mt * 128:(mt + 1) * 128], qx[:, h * 4 + cc],
                            start=(cc == 0), stop=(cc == 3)), reads=[kxb, qxb], writes=[psb[pi]])
                    P.op("act", lambda pi=pi, mt=mt: nc.scalar.activation(out=Ex[:, mt], in_=ps[pi][:], func=AF.Exp,
                                                                           scale=scale), reads=[psb[pi]], writes=[Exb])
                pi = nextps()
                for mt in range(MT):
                    P.op("pe", lambda pi=pi, mt=mt: nc.tensor.matmul(ps[pi][:], ones_bf[:], Ex[:, mt],
                                                                     start=(mt == 0), stop=(mt == MT - 1)),
                         reads=[Exb, cB], writes=[psb[pi]])
                P.op("dve", lambda pi=pi: nc.vector.reciprocal(out=rden, in_=ps[pi][:]), reads=[psb[pi]], writes=[rdb])
                for dt in range(4):
                    pi = nextps()
                    for mt in range(MT):
                        P.op("pe", lambda pi=pi, mt=mt, dt=dt, h=h: nc.tensor.matmul(
                            ps[pi][:], vx[:, mt, (h * 4 + dt) * 128:(h * 4 + dt + 1) * 128], Ex[:, mt],
                            start=(mt == 0), stop=(mt == MT - 1)), reads=[vxb, Exb], writes=[psb[pi]])
                    P.op("dve", lambda pi=pi, dt=dt, h=h: nc.vector.tensor_tensor(out=ox[:, h * 4 + dt], in0=ps[pi][:],
                                                                                  in1=rden, op=ALU.mult),
                         reads=[psb[pi], rdb], writes=[oxb])
            proj_fm("xattn_wo", l, 0, KC, ox, oxb, KC, evac_fT)
            post_residual(5, l, False, xres, xb[tb], tb)
        nws[0] = NWS

    xinb = [Buf("xin%d" % t) for t in range(NTB)]
    outb = [Buf("out%d" % t) for t in range(NTB)]
    stop = cfg.get("stop", 4)
    for l in range(DEPTH):
        ffn(l, "ffn1", xT_in if l == 0 else xres, xinb if l == 0 else xb, xres, xb, 0, 0)
        if stop >= 2:
            mix_in(l, xres, xb)
            mlstm(l)
            diffattn(l)
            mix_out(l)
        if stop >= 3:
            xattn(l)
        last = l == DEPTH - 1
        if stop >= 4:
            ffn(l, "ffn2", xres, xb, outT if last else xres, outb if last else xb, 7, 1)
    if stop < 4:
        P.barrier()
        for tb in range(NTB):
            P.dma("sp", outT[:, tb * TB:(tb + 1) * TB], xres[:, tb * TB:(tb + 1) * TB], reads=[xb[tb]],
                  writes=[outb[tb]])
    P.emit()
    return nc, es


def make_in_maps(cfg, inputs):
    S, D, DFF, DEPTH, MEM = cfg["S"], cfg["D"], cfg["DFF"], cfg["DEPTH"], cfg["MEM"]
    f = lambda a: np.ascontiguousarray(np.asarray(a, dtype=np.float32))
    maps = []
    wnames0 = ["ffn1_w_gate", "ffn1_w_up", "w_in", "w_out", "xattn_wq", "xattn_wk", "xattn_wv", "xattn_wo",
               "ffn2_w_gate", "ffn2_w_up"]
    wnames1 = ["ffn1_w_down", "ffn2_w_down"]
    smalls = ["ffn1_norm_pre", "ffn1_norm_post", "mix_norm_pre", "mix_norm_post", "xattn_norm_pre",
              "xattn_norm_post", "mem_norm", "ffn2_norm_pre", "ffn2_norm_post", "conv_b", "b_igate", "b_fgate",
              "mlstm_norm", "diff_subln", "conv_w", "diff_lambda", "rel_bias"]
    for c in range(NCORES):
        m = {"xT": f(np.asarray(inputs["x"][c]).T), "memT": f(np.asarray(inputs["mem"][c]).T)}
        for k in smalls:
            m[k] = f(inputs[k])
        for l in range(DEPTH):
            for nm in wnames0:
                w = np.asarray(inputs[nm][l])
                r = w.shape[0] // 8
                m["%s_%d" % (nm, l)] = f(w[c * r:(c + 1) * r, :])
            for nm in wnames1:
                w = np.asarray(inputs[nm][l])
                r = w.shape[1] // 8
                m["%s_%d" % (nm, l)] = f(w[:, c * r:(c + 1) * r])
        maps.append(m)
    return maps


def kernel(**inputs):
    cfg = CFG
    nc, es = build(cfg)
    with es:
        maps = make_in_maps(cfg, inputs)
        res = run_bass_kernel_spmd(nc, maps, core_ids=list(range(NCORES)))
    global LAST
    LAST = res.results
    out = np.stack([np.ascontiguousarray(res.results[c]["outT"].T) for c in range(NCORES)], axis=0)
    return out.astype(np.float32)
```
